# Optimizing a Trainium2 kernel written in Bass

```python
import jax
import jax.numpy as jnp
from jax import lax
import numpy as np


D_MODEL = 2048
BATCH = 2
SEQ = 16384
DEPTH = 2

CHUNK = 64
D_MIX = D_MODEL
N_GROUPS = 4
D_GROUP = D_MIX // N_GROUPS
POOL_WINDOWS = (2, 4, 8, 16)
POOL_CH = D_GROUP // len(POOL_WINDOWS)
SGU_BLOCK = 128
SGU_HEADS = 4
SGU_HEAD_DIM = D_GROUP // SGU_HEADS
CONV_WIDTH = 31
HGRN_HEADS = 4
HGRN_HEAD_DIM = D_GROUP // HGRN_HEADS
FORGET_FLOOR = 1e-20
D_FF = 4 * D_MODEL
EPS = 1e-6
IN_A = D_GROUP
IN_B = 2 * D_GROUP
IN_C = 2 * D_GROUP
IN_D = 4 * D_GROUP
D_IN = IN_A + IN_B + IN_C + IN_D
N_MOD = 6

kernel_name = 'hybrid_pool_sgu_conv_hgrn2_encoder'


def rms_norm(x, g):
    xf = x.astype(jnp.float32)
    y = xf * lax.rsqrt(jnp.mean(xf * xf, axis=-1, keepdims=True) + EPS)
    return (y * g.astype(jnp.float32)).astype(x.dtype)


def layer_norm(x, g, b):
    xf = x.astype(jnp.float32)
    mu = jnp.mean(xf, axis=-1, keepdims=True)
    var = jnp.mean(jnp.square(xf - mu), axis=-1, keepdims=True)
    y = (xf - mu) * lax.rsqrt(var + EPS)
    return (y * g.astype(jnp.float32) + b.astype(jnp.float32)).astype(x.dtype)


def pool_mixer(xa, w, scale):
    B, T, _ = xa.shape
    xf = xa.astype(jnp.float32).reshape(B, T, len(POOL_WINDOWS), POOL_CH)
    cs = jnp.cumsum(xf, axis=1)
    t = jnp.arange(T)
    outs = []
    for gi, win in enumerate(POOL_WINDOWS):
        c = cs[:, :, gi]
        lag = jnp.pad(c, ((0, 0), (win, 0), (0, 0)))[:, :T]
        cnt = jnp.minimum(t + 1, win).astype(jnp.float32)[None, :, None]
        outs.append((c - lag) / cnt)
    pooled = (jnp.stack(outs, axis=2) - xf).astype(xa.dtype)
    y = jnp.einsum('btgc,gcd->btgd', pooled, w)
    return y.reshape(B, T, D_GROUP) * scale


def sgu_mixer(xb, norm_g, norm_b, ws, bs):
    B, T, _ = xb.shape
    u, v = jnp.split(xb, 2, axis=-1)
    v = layer_norm(v, norm_g, norm_b)
    n = T // SGU_BLOCK
    v = v.reshape(B, n, SGU_BLOCK, SGU_HEADS, SGU_HEAD_DIM)
    pos = jnp.arange(SGU_BLOCK) // CHUNK
    mask = pos[None, :] <= pos[:, None]
    wm = jnp.where(mask[None], ws, jnp.zeros_like(ws))
    mixed = jnp.einsum('hij,bnjhd->bnihd', wm, v) + bs.T[None, None, :, :, None]
    return u * mixed.reshape(B, T, D_GROUP)


def conv_module(xc, dw, dw_b, ng, nb, pw, pw_b):
    a, g = jnp.split(xc, 2, axis=-1)
    h = a * jax.nn.sigmoid(g)
    h = lax.conv_general_dilated(
        h, dw[:, None, :], window_strides=(1,), padding=[(CONV_WIDTH - 1, 0)],
        dimension_numbers=('NWC', 'WIO', 'NWC'), feature_group_count=D_GROUP) + dw_b
    h = jax.nn.silu(layer_norm(h, ng, nb))
    return h @ pw + pw_b


def hgrn_lower_bounds(logits):
    sm = jax.nn.softmax(logits.astype(jnp.float32), axis=0)
    return jnp.cumsum(sm, axis=0) - sm[0:1]


def hgrn2_mixer(xd, lb, norm_g):
    B, T, _ = xd.shape
    q, fz, iv, og = jnp.split(xd, 4, axis=-1)
    fz = fz.astype(jnp.float32)
    f = lb + (1.0 - lb) * jax.nn.sigmoid(fz)
    logf = jnp.log(jnp.maximum(f, FORGET_FLOOR))
    k = (1.0 - lb) * jax.nn.sigmoid(-fz)
    n = T // CHUNK

    def to_chunks(a):
        a = a.astype(jnp.float32).reshape(B, n, CHUNK, HGRN_HEADS, HGRN_HEAD_DIM)
        return jnp.transpose(a, (1, 0, 3, 2, 4))

    qc = to_chunks(q) * (HGRN_HEAD_DIM ** -0.5)
    kc = to_chunks(k)
    vc = to_chunks(iv)
    bc = jnp.cumsum(to_chunks(logf), axis=3)
    tri = jnp.tril(jnp.ones((CHUNK, CHUNK), dtype=bool))[None, None, :, :, None]

    def step(S, inp):
        qq, kk, vv, bb = inp
        inter = jnp.einsum('bhck,bhkv->bhcv', qq * jnp.exp(bb), S)
        diff = bb[:, :, :, None, :] - bb[:, :, None, :, :]
        decay = jnp.where(tri, jnp.exp(jnp.minimum(diff, 0.0)), 0.0)
        att = jnp.einsum('bhik,bhjk,bhijk->bhij', qq, kk, decay)
        intra = jnp.einsum('bhij,bhjv->bhiv', att, vv)
        b_last = bb[:, :, -1]
        kd = kk * jnp.exp(jnp.minimum(b_last[:, :, None, :] - bb, 0.0))
        S = jnp.exp(b_last)[..., None] * S + jnp.einsum('bhck,bhcv->bhkv', kd, vv)
        return S, inter + intra

    S0 = jnp.zeros((B, HGRN_HEADS, HGRN_HEAD_DIM, HGRN_HEAD_DIM), jnp.float32)
    _, o = lax.scan(step, S0, (qc, kc, vc, bc))
    o = jnp.transpose(o, (1, 0, 3, 2, 4)).reshape(B, T, HGRN_HEADS, HGRN_HEAD_DIM)
    o = o * lax.rsqrt(jnp.mean(o * o, axis=-1, keepdims=True) + EPS)
    o = o * norm_g.astype(jnp.float32).reshape(HGRN_HEADS, HGRN_HEAD_DIM)
    return o.reshape(B, T, D_GROUP).astype(xd.dtype) * jax.nn.silu(og)


def setup_inputs(seed: int = 0) -> dict:
    key = jax.random.key(seed)
    ks = jax.random.split(key, 32)
    L, D = DEPTH, D_MODEL
    nrm = lambda k, shape, s: jax.random.normal(k, shape, jnp.float32) * s
    gain = lambda k, shape: 1.0 + nrm(k, shape, 0.05)
    return {
        'x': nrm(ks[0], (BATCH, SEQ, D), 1.0),
        'c': nrm(ks[1], (BATCH, D), 1.0),
        'ada_w': nrm(ks[2], (L, D, N_MOD * D), 0.5 * D ** -0.5),
        'ada_b': nrm(ks[3], (L, N_MOD * D), 0.01),
        'norm_mix_pre': gain(ks[4], (L, D)),
        'norm_mix_post': gain(ks[5], (L, D)),
        'norm_mlp_pre': gain(ks[6], (L, D)),
        'norm_mlp_post': gain(ks[7], (L, D)),
        'w_in': nrm(ks[8], (L, D, D_IN), D ** -0.5),
        'pool_w': nrm(ks[9], (L, len(POOL_WINDOWS), POOL_CH, POOL_CH), POOL_CH ** -0.5),
        'pool_scale': 1.0 + nrm(ks[10], (L, D_GROUP), 0.1),
        'sgu_norm_g': gain(ks[11], (L, D_GROUP)),
        'sgu_norm_b': nrm(ks[12], (L, D_GROUP), 0.01),
        'sgu_w': nrm(ks[13], (L, SGU_HEADS, SGU_BLOCK, SGU_BLOCK), SGU_BLOCK ** -0.5),
        'sgu_b': 1.0 + nrm(ks[14], (L, SGU_HEADS, SGU_BLOCK), 0.05),
        'conv_dw': nrm(ks[15], (L, CONV_WIDTH, D_GROUP), CONV_WIDTH ** -0.5),
        'conv_dw_b': nrm(ks[16], (L, D_GROUP), 0.01),
        'conv_norm_g': gain(ks[17], (L, D_GROUP)),
        'conv_norm_b': nrm(ks[18], (L, D_GROUP), 0.01),
        'conv_pw': nrm(ks[19], (L, D_GROUP, D_GROUP), D_GROUP ** -0.5),
        'conv_pw_b': nrm(ks[20], (L, D_GROUP), 0.01),
        'hgrn_lb_logits': nrm(ks[21], (L, D_GROUP), 0.5),
        'hgrn_norm_g': gain(ks[22], (L, D_GROUP)),
        'w_out': nrm(ks[23], (L, D_MIX, D), D_MIX ** -0.5),
        'mlp_w1': nrm(ks[24], (L, D, D_FF), D ** -0.5),
        'mlp_w2': nrm(ks[25], (L, D_FF, D), D_FF ** -0.5),
    }


def reference(x, c, ada_w, ada_b, norm_mix_pre, norm_mix_post, norm_mlp_pre, norm_mlp_post,
              w_in, pool_w, pool_scale, sgu_norm_g, sgu_norm_b, sgu_w, sgu_b,
              conv_dw, conv_dw_b, conv_norm_g, conv_norm_b, conv_pw, conv_pw_b,
              hgrn_lb_logits, hgrn_norm_g, w_out, mlp_w1, mlp_w2):
    lb_all = hgrn_lower_bounds(hgrn_lb_logits)
    cond = jax.nn.silu(c)
    for l in range(DEPTH):
        mod = (cond @ ada_w[l] + ada_b[l])[:, None, :]
        sh_m, sc_m, gt_m, sh_f, sc_f, gt_f = jnp.split(mod, N_MOD, axis=-1)
        h = rms_norm(x, norm_mix_pre[l]) * (1.0 + sc_m) + sh_m
        z = h @ w_in[l]
        za, zb, zc, zd = jnp.split(z, [IN_A, IN_A + IN_B, IN_A + IN_B + IN_C], axis=-1)
        ya = pool_mixer(za, pool_w[l], pool_scale[l])
        yb = sgu_mixer(zb, sgu_norm_g[l], sgu_norm_b[l], sgu_w[l], sgu_b[l])
        yc = conv_module(zc, conv_dw[l], conv_dw_b[l], conv_norm_g[l], conv_norm_b[l],
                         conv_pw[l], conv_pw_b[l])
        yd = hgrn2_mixer(zd, lb_all[l], hgrn_norm_g[l])
        y = jnp.concatenate([ya, yb, yc, yd], axis=-1) @ w_out[l]
        x = x + gt_m * rms_norm(y, norm_mix_post[l])
        h = rms_norm(x, norm_mlp_pre[l]) * (1.0 + sc_f) + sh_f
        y = jnp.square(jax.nn.relu(h @ mlp_w1[l])) @ mlp_w2[l]
        x = x + gt_f * rms_norm(y, norm_mlp_post[l])
    return x
```

```python
import contextlib
import numpy as np
import concourse.bass as bass
import concourse.mybir as mybir
from concourse.bass_utils import run_bass_kernel_spmd

F32 = mybir.dt.float32
BF16 = mybir.dt.bfloat16
AF = mybir.ActivationFunctionType
ALU = mybir.AluOpType
AX = mybir.AxisListType

D = 2048
SEQ = 16384
BATCH = 2
NCORE = 8
TOK = SEQ * BATCH // NCORE
T = 256
H = 32
TW = T + H
NCH = T // 128
KC = D // 128
NV = 104
EPS = 1e-6
NB = 2
import os
STOP = float(os.environ.get('MK_STOP', '99'))


class StopBuild(Exception):
    pass


class Buf:
    __slots__ = ("name", "w", "r")

    def __init__(self, name=""):
        self.name = name
        self.w = None
        self.r = []


PSUM_BUFS = {"pm0", "pm1", "pst", "patt", "po0", "po1", "pS", "ptr"}


class Op:
    __slots__ = ("eng", "fn", "deps", "needed", "val", "sem", "dma_key", "idx")


class Sched:
    ENGS = ("pe", "act", "dve", "pool", "sp")

    def __init__(self, nc):
        self.nc = nc
        self.ops = {e: [] for e in self.ENGS}
        self.nops = 0

    def add(self, eng, fn, reads=(), writes=(), dma_key=None):
        op = Op()
        op.eng = eng
        op.fn = fn
        op.deps = set()
        op.needed = False
        op.val = None
        op.sem = None
        op.dma_key = dma_key
        op.idx = self.nops
        self.nops += 1
        for b in reads:
            if b.w is not None:
                op.deps.add(b.w)
            if b.name in PSUM_BUFS:
                for r in b.r:
                    if r.eng != eng:
                        op.deps.add(r)
        for b in writes:
            if b.w is not None:
                op.deps.add(b.w)
            for r in b.r:
                op.deps.add(r)
        for b in reads:
            b.r.append(op)
        for b in writes:
            b.w = op
            b.r = []
        op.deps.discard(op)
        for d in op.deps:
            d.needed = True
        self.ops[eng].append(op)
        return op

    def emit(self, final_wait_ops=()):
        nc = self.nc
        fin = Op()
        fin.eng = "sp"
        fin.fn = None
        fin.deps = set(final_wait_ops)
        fin.needed = False
        fin.val = None
        fin.sem = None
        fin.dma_key = None
        fin.idx = self.nops
        for d in fin.deps:
            d.needed = True
        self.ops["sp"].append(fin)
        dma_keys = []
        for e in self.ENGS:
            for op in self.ops[e]:
                if op.dma_key is not None and op.dma_key not in dma_keys:
                    dma_keys.append(op.dma_key)
        with contextlib.ExitStack() as st:
            esem = {e: st.enter_context(nc.semaphore("s_" + e)) for e in self.ENGS}
            ksem = {k: st.enter_context(nc.semaphore("d_" + str(k))) for k in dma_keys}
            cnt = {}
            for e in self.ENGS:
                for op in self.ops[e]:
                    if op.dma_key is not None:
                        op.sem = ksem[op.dma_key]
                        c = cnt.get(("k", op.dma_key), 0) + 16
                        cnt[("k", op.dma_key)] = c
                        op.val = c
                    elif op.needed:
                        op.sem = esem[e]
                        c = cnt.get(("e", e), 0) + 1
                        cnt[("e", e)] = c
                        op.val = c
            block = st.enter_context(nc.Block())

            def run(e, eng):
                waited = {}
                for op in self.ops[e]:
                    req = {}
                    for d in op.deps:
                        if d.eng == "pe" and e == "pe" and d.dma_key is None:
                            continue
                        key = d.sem.num
                        if key not in req or req[key][1] < d.val:
                            req[key] = (d.sem, d.val)
                    for key, (sem, val) in req.items():
                        if waited.get(key, 0) < val:
                            eng.wait_ge(sem, val)
                            waited[key] = val
                    if op.fn is None:
                        continue
                    ins = op.fn(eng)
                    if op.dma_key is not None:
                        ins.then_inc(op.sem, 16)
                    elif op.needed:
                        ins.then_inc(op.sem, 1)

            @block.tensor
            def _(eng):
                run("pe", eng)

            @block.scalar
            def _(eng):
                run("act", eng)

            @block.vector
            def _(eng):
                run("dve", eng)

            @block.gpsimd
            def _(eng):
                run("pool", eng)

            @block.sync
            def _(eng):
                run("sp", eng)


def build(mode, NT=TOK // T, debug=False):
    full = mode == "B"
    nc = bass.Bass("TRN2", target_bir_lowering=False)

    def din(name, shape):
        return nc.dram_tensor(name, shape, F32, kind="ExternalInput").ap()

    def dout(name, shape):
        return nc.dram_tensor(name, shape, F32, kind="ExternalOutput").ap()

    xT = din("xT", [D, H + TOK])
    cT = din("cT", [128, KC])
    ada_w = din("ada_w", [D, 6 * D])
    ada_bT = din("ada_bT", [128, 96])
    vecs_d = din("vecs", [128, NV])
    lsel_d = din("lsel", [128, 1])
    w_in = din("w_in", [D, 4608])
    if full:
        w_out = din("w_out", [D, D])
        w1 = din("w1", [D, 4 * D])
        w2 = din("w2", [4 * D, D])
        pool_w_d = din("pool_w", [128, 4, 128])
        sgu_wT_d = din("sgu_wT", [128, 4, 128])
        sgu_bb_d = din("sgu_bb", [128, 4, 128])
        dwT_d = din("dwT", [128, 4, 31])
        pw_d = din("conv_pw", [512, 512])
        Sp_d = din("Sp", [3, 128, 512])
        Lp_d = din("Lp", [3, 128, 4])
        nf_d = din("nf", [128, 1])
        corr_d = din("corr", [128, 4, 16])
        xo_d = dout("xo", [D, TOK])
    else:
        So_d = dout("So", [128, 512])
        Lo_d = dout("Lo", [128, 4])
    dbg_outs = {}

    with contextlib.ExitStack() as st:
        def sb(name, shape, dt=F32):
            return st.enter_context(nc.sbuf_tensor("sb_" + name, shape, dt))

        def ps(name, shape, dt=F32):
            return st.enter_context(nc.psum_tensor("ps_" + name, shape, dt))

        s = Sched(nc)
        bufs = {}

        def B(name):
            if name not in bufs:
                bufs[name] = Buf(name)
            return bufs[name]

        def A(eng, fn, r=(), w=(), key=None):
            return s.add(eng, fn, reads=[B(x) if isinstance(x, str) else x for x in r],
                         writes=[B(x) if isinstance(x, str) else x for x in w], dma_key=key)

        xt = sb("xt", [128, KC, TW])
        hT = sb("hT", [128, KC, TW], BF16)
        sq = sb("sq", [128, KC, TW], BF16)
        wb = [sb(f"wb{i}", [128, KC, 512], BF16) for i in range(NB)]
        vecs = sb("vecs", [128, NV])
        modT = sb("modT", [128, 96])
        abT = sb("abT", [128, 96])
        mods = sb("mods", [128, 6, KC])
        cTs = sb("cTs", [128, KC])
        cond = sb("cond", [128, KC], BF16)
        ones = sb("ones", [128, 128], BF16)
        epsT = sb("epsT", [128, 1])
        lselT = sb("lselT", [128, 1])
        lbv = sb("lbv", [128, 4, 4])
        rst128 = sb("rst128", [128, T])
        rstd = sb("rstd", [128, TW])
        tmpf = sb("tmpf", [128, TW])
        S = sb("S", [128, 512])
        Sbf = sb("Sbf", [128, 512], BF16)
        Lacc = sb("Lacc", [128, 4])
        ident = sb("ident", [128, 128], BF16)
        identf = sb("identf", [128, 128])
        sig = sb("sig", [128, T])
        flog = sb("flog", [128, T])
        kf = sb("kf", [128, T])
        b128 = sb("b128", [128, T])
        d1 = sb("d1", [128, T])
        kdT = sb("kdT", [128, T], BF16)
        kd_sb = sb("kd_sb", [128, NCH, 128], BF16)
        ivtok = sb("ivtok", [128, NCH, 512], BF16)
        decs = sb("decs", [128, NCH])
        blsum = sb("blsum", [128, 1])
        if not full:
            sigA = sb("sigA", [128, 4, T])
        if full:
            yacc = sb("yacc", [128, KC, T])
            ymix = sb("ymix", [128, KC, T], BF16)
            hidq = [sb(f"hidq{i}", [128, KC, T], BF16) for i in range(2)]
            rl = [sb(f"rl{i}", [128, T], BF16) for i in range(4)]
            F4 = [sb(f"F4_{i}", [128, 4, TW]) for i in range(5)]
            B4 = [sb(f"B4_{i}", [128, 4, TW], BF16) for i in range(3)]
            F1 = [sb(f"F1_{i}", [128, TW]) for i in range(4)]
            vn = sb("vn", [128, NCH, 512], BF16)
            vsq = sb("vsq", [128, 512])
            vraw = sb("vraw", [128, 512])
            st4 = sb("st4", [128, 8])
            poolw = sb("poolw", [128, 4, 128], BF16)
            wmT = sb("wmT", [128, 4, 128], BF16)
            sgbb = sb("sgbb", [128, 4, 128])
            Csg = sb("Csg", [128, 4, 128])
            dwT = sb("dwT", [128, 4, 31])
            pwb = sb("pwb", [128, 4, 512], BF16)
            nfT = sb("nfT", [128, 1])
            corr = sb("corr", [128, 4, 16])
            triM = sb("triM", [128, 128])
            rst16 = sb("rst16", [128, T])
            Spt = sb("Spt", [128, 512])
            Lpt = sb("Lpt", [128, 4])
            b16 = sb("b16", [128, T])
            qs = sb("qs", [128, T])
            Qc = sb("Qc", [128, T], BF16)
            Qrel = sb("Qrel", [128, T], BF16)
            Kref = sb("Kref", [128, 8, NCH, 128], BF16)
            kt1 = sb("kt1", [128, NCH, 128])
            kt2 = sb("kt2", [128, NCH, 128])
            attT = sb("attT", [128, NCH, 128], BF16)
            osq = sb("osq", [128, T], BF16)
            sog = sb("sog", [128, T])
        pm = [ps(f"pm{i}", [128, 512]) for i in range(2)]
        pst = ps("pst", [128, 512])
        patt = ps("patt", [128, 512])
        po = [ps(f"po{i}", [128, 512]) for i in range(2)]
        pS = ps("pS", [128, 512])
        ptr = ps("ptr", [128, 512], BF16)
        PMB = [B("pm0"), B("pm1")]
        pmi = [0]

        def next_pm():
            i = pmi[0] % 2
            pmi[0] += 1
            return pm[i], PMB[i]

        wlist = []
        wstate = {"n": 0, "issued": 0}
        WB = [B(f"wb{i}") for i in range(NB)]

        def std_block(w_ap, r0, c0):
            return w_ap[r0:r0 + D, c0:c0 + 512].rearrange("(k p) c -> p k c", p=128)

        def issue_load(n):
            src = wlist[n]
            slot = n % NB
            A("pool", lambda e, o=wb[slot], i=src: e.dma_start(out=o[:], in_=i), w=[WB[slot]], key=f"w{slot}")

        def wnext():
            n = wstate["n"]
            while wstate["issued"] <= min(n + NB - 1, len(wlist) - 1):
                issue_load(wstate["issued"])
                wstate["issued"] += 1
            wstate["n"] += 1
            return wb[n % NB], WB[n % NB]

        ADA_BLOCKS = list(range(24)) if full else list(range(4))
        if not full:
            ADA_BLOCKS = list(range(8))
        for j in ADA_BLOCKS:
            wlist.append(std_block(ada_w, 0, j * 512))
        WIN_ORDER = [0, 1, 2, 4, 3, 5, 6, 7, 8] if full else [6, 7]
        for i in range(NT):
            for jb in WIN_ORDER:
                wlist.append(std_block(w_in, 0, jb * 512))
            if full:
                for cb in range(4):
                    wlist.append(std_block(w_out, 0, cb * 512))
                for q in range(4):
                    for cbw in range(4):
                        wlist.append(std_block(w1, 0, (4 * q + cbw) * 512))
                    for cb in range(4):
                        wlist.append(std_block(w2, q * D, cb * 512))

        def mmg(out, pairs, reads, writes):
            def fn(e, out=out, pairs=pairs):
                n = len(pairs)
                ins = None
                for i, (l, r) in enumerate(pairs):
                    ins = e.matmul(out, lhsT=l, rhs=r, start=(i == 0), stop=(i == n - 1))
                return ins
            return A("pe", fn, r=reads, w=writes)

        def dma_in(dst, src, bname, key):
            return A("sp", lambda e, d=dst, s_=src: e.dma_start(out=d, in_=s_), w=[bname], key=key)

        def dbg(name, ap, shape, bnames, dt=F32):
            if not debug:
                return
            o = nc.dram_tensor("dbg_" + name, shape, dt, kind="ExternalOutput").ap()
            dbg_outs[name] = A("sp", lambda e, o=o, a=ap: e.dma_start(out=o, in_=a), r=bnames, key="dbg_" + name)

        def rstd_from(psrc, n, scale, rname, out_ap):
            A("act", lambda e, o=out_ap, i=psrc: e.activation(out=o, in_=i, func=AF.Sqrt, bias=epsT[:, 0:1], scale=scale),
              r=[rname, "epsT"], w=["rstd_tmp"])
            A("dve", lambda e, o=out_ap: e.reciprocal(out=o, in_=o), r=["rstd_tmp"], w=["rstd_tmp"])

        dma_in(vecs[:], vecs_d, "vecs", "c0")
        dma_in(cTs[:], cT, "cTs", "c1")
        dma_in(abT[:], ada_bT, "abT", "c2")
        dma_in(lselT[:], lsel_d, "lselT", "c3")
        A("dve", lambda e: e.memset(ones[:], 1.0), w=["ones"])
        A("dve", lambda e: e.memset(epsT[:], EPS), w=["epsT"])
        A("dve", lambda e: e.memset(S[:], 0.0), w=["S"])
        A("dve", lambda e: e.memset(Lacc[:], 0.0), w=["Lacc"])
        A("dve", lambda e: e.memset(rst128[:], 1.0), w=["rst128"])
        A("dve", lambda e: e.memset(rst128[:].rearrange("p (c t) -> p c t", t=128)[:, :, 0:1], 0.0), w=["rst128"])
        A("pool", lambda e: e.memset(identf[:], 1.0), w=["identf"])
        A("pool", lambda e: e.affine_select(out=identf[:], in_=identf[:], pattern=[[-1, 128]], compare_op=ALU.is_equal,
                                            fill=0.0, base=0, channel_multiplier=1), r=["identf"], w=["identf"])
        A("dve", lambda e: e.tensor_copy(out=ident[:], in_=identf[:]), r=["identf"], w=["ident"])
        if full:
            dma_in(sgbb[:], sgu_bb_d, "sgbb", "c4")
            dma_in(dwT[:], dwT_d, "dwT", "c5")
            dma_in(nfT[:], nf_d, "nfT", "c6")
            dma_in(corr[:], corr_d, "corr", "c7")
            A("pool", lambda e: e.dma_start(out=poolw[:], in_=pool_w_d), w=["poolw"], key="c8")
            A("pool", lambda e: e.dma_start(out=wmT[:], in_=sgu_wT_d), w=["wmT"], key="c9")
            A("pool", lambda e: e.dma_start(out=pwb[:], in_=pw_d.rearrange("(k p) c -> p k c", p=128)), w=["pwb"], key="c10")
            A("dve", lambda e: e.memset(rst16[:], 1.0), w=["rst16"])
            A("dve", lambda e: e.memset(rst16[:].rearrange("p (c t) -> p c t", t=16)[:, :, 0:1], 0.0), w=["rst16"])
            A("pool", lambda e: e.memset(triM[:], 1.0), w=["triM"])
            A("pool", lambda e: e.affine_select(out=triM[:], in_=triM[:], pattern=[[1, 128]], compare_op=ALU.is_ge,
                                                fill=0.0, base=0, channel_multiplier=-1), r=["triM"], w=["triM"])
            A("dve", lambda e: e.memset(wmT[64:128, :, 0:64], 0.0), r=[], w=["wmT"])
            A("dve", lambda e: e.memset(patt[:], 0.0), w=["patt"])
        A("act", lambda e: e.activation(out=cond[:], in_=cTs[:], func=AF.Silu), r=["cTs"], w=["cond"])
        for jn, j in enumerate(ADA_BLOCKS):
            blk, blkB = wnext()
            for m in range(4):
                col = j * 4 + m
                mmg(pst[:, col:col + 1], [(blk[:, k, m * 128:(m + 1) * 128], cond[:, k:k + 1]) for k in range(KC)],
                    [blkB, "cond"], ["pst"])
        nmod = len(ADA_BLOCKS) * 4
        A("dve", lambda e: e.tensor_tensor(out=modT[:, 0:nmod], in0=pst[:, 0:nmod], in1=abT[:, 0:nmod], op=ALU.add),
          r=["pst", "abT"], w=["modT"])
        A("dve", lambda e: e.scalar_tensor_tensor(out=mods[:, 0, :], in0=modT[:, 16:32], scalar=1.0, in1=vecs[:, 0:16],
                                                  op0=ALU.add, op1=ALU.mult), r=["modT", "vecs"], w=["mods"])
        A("dve", lambda e: e.tensor_copy(out=mods[:, 1, :], in_=modT[:, 0:16]), r=["modT"], w=["mods"])
        if full:
            A("dve", lambda e: e.tensor_tensor(out=mods[:, 2, :], in0=modT[:, 32:48], in1=vecs[:, 16:32], op=ALU.mult),
              r=["modT", "vecs"], w=["mods"])
            A("dve", lambda e: e.scalar_tensor_tensor(out=mods[:, 3, :], in0=modT[:, 64:80], scalar=1.0, in1=vecs[:, 32:48],
                                                      op0=ALU.add, op1=ALU.mult), r=["modT", "vecs"], w=["mods"])
            A("dve", lambda e: e.tensor_copy(out=mods[:, 4, :], in_=modT[:, 48:64]), r=["modT"], w=["mods"])
            A("dve", lambda e: e.tensor_tensor(out=mods[:, 5, :], in0=modT[:, 80:96], in1=vecs[:, 48:64], op=ALU.mult),
              r=["modT", "vecs"], w=["mods"])
        A("act", lambda e: e.activation(out=lbv[:, 2, :], in_=vecs[:, 92:96], func=AF.Exp), r=["vecs"], w=["lbv"])
        A("act", lambda e: e.activation(out=lbv[:, 3, :], in_=vecs[:, 96:100], func=AF.Exp), r=["vecs"], w=["lbv"])
        A("dve", lambda e: e.tensor_tensor(out=lbv[:, 2, :], in0=lbv[:, 2, :], in1=lbv[:, 3, :], op=ALU.add), r=["lbv"], w=["lbv"])
        A("dve", lambda e: e.reciprocal(out=lbv[:, 2, :], in_=lbv[:, 2, :]), r=["lbv"], w=["lbv"])
        A("dve", lambda e: e.scalar_tensor_tensor(out=lbv[:, 0, :], in0=lbv[:, 3, :], scalar=lselT[:, 0:1], in1=lbv[:, 2, :],
                                                  op0=ALU.mult, op1=ALU.mult), r=["lbv", "lselT"], w=["lbv"])
        A("dve", lambda e: e.tensor_scalar(out=lbv[:, 1, :], in0=lbv[:, 0, :], scalar1=-1.0, scalar2=1.0, op0=ALU.mult, op1=ALU.add),
          r=["lbv"], w=["lbv"])
        A("dve", lambda e: e.tensor_scalar(out=lbv[:, 2, :], in0=lbv[:, 1, :], scalar1=-1.0, scalar2=None, op0=ALU.mult),
          r=["lbv"], w=["lbv"])
        if full:
            for h in range(4):
                mmg(pm[0][:, h * 128:(h + 1) * 128], [(ones[:], wmT[:, h, :])], ["ones", "wmT"], [PMB[0]])
            for h in range(4):
                A("dve", lambda e, h=h: e.scalar_tensor_tensor(out=Csg[:, h, :], in0=pm[0][:, h * 128:(h + 1) * 128],
                                                               scalar=vecs[:, 72 + h:73 + h], in1=sgbb[:, h, :],
                                                               op0=ALU.mult, op1=ALU.add),
                  r=[PMB[0], "vecs", "sgbb"], w=["Csg"])
            for m_ in range(3):
                dma_in(Spt[:], Sp_d[m_], "Spt", "c11")
                dma_in(Lpt[:], Lp_d[m_], "Lpt", "c12")
                A("act", lambda e: e.activation(out=Lpt[:], in_=Lpt[:], func=AF.Exp), r=["Lpt"], w=["Lpt"])
                A("dve", lambda e: e.tensor_tensor(out=S[:].rearrange("p (h v) -> p h v", h=4), in0=S[:].rearrange("p (h v) -> p h v", h=4),
                                                   in1=Lpt[:, :, None].to_broadcast([128, 4, 128]), op=ALU.mult), r=["S", "Lpt"], w=["S"])
                A("dve", lambda e: e.tensor_tensor(out=S[:], in0=S[:], in1=Spt[:], op=ALU.add), r=["S", "Spt"], w=["S"])
        A("act", lambda e: e.copy(out=Sbf[:], in_=S[:]), r=["S"], w=["Sbf"])

        def norm_front(gi, si, ncols, c_lo):
            cs = slice(c_lo, c_lo + ncols)
            for g4 in range(4):
                A("act", lambda e, g4=g4: e.activation(out=sq[:, 4 * g4:4 * g4 + 4, cs], in_=xt[:, 4 * g4:4 * g4 + 4, cs], func=AF.Square),
                  r=["xt"], w=[f"sq{g4}"])
            mmg(pst[:, 0:ncols], [(ones[:], sq[:, k, cs]) for k in range(KC)], ["ones"] + [f"sq{g}" for g in range(4)], ["pst"])
            A("act", lambda e: e.activation(out=rstd[:, 0:ncols], in_=pst[:, 0:ncols], func=AF.Sqrt, bias=epsT[:, 0:1], scale=1.0 / D),
              r=["pst", "epsT"], w=["rstd"])
            A("dve", lambda e: e.reciprocal(out=rstd[:, 0:ncols], in_=rstd[:, 0:ncols]), r=["rstd"], w=["rstd"])
            for k in range(KC):
                eng = "dve" if k % 2 == 0 else "pool"
                A(eng, lambda e, k=k: e.tensor_tensor(out=xt_tmp[k % 2][:, 0:ncols], in0=xt[:, k, cs], in1=rstd[:, 0:ncols], op=ALU.mult),
                  r=["xt", "rstd"], w=[f"xtt{k % 2}"])
                A("act", lambda e, k=k: e.activation(out=hT[:, k, cs], in_=xt_tmp[k % 2][:, 0:ncols], func=AF.Identity,
                                                     bias=mods[:, si, k:k + 1], scale=mods[:, gi, k:k + 1]),
                  r=[f"xtt{k % 2}", "mods"], w=[f"hT{k}"])

        xt_tmp = [sb("xtt0", [128, TW]), sb("xtt1", [128, TW])]
        HTB = [f"hT{k}" for k in range(KC)]

        def fm_block(blk, blkB, m, cs, n):
            p, pB = next_pm()
            mmg(p[:, 0:n], [(blk[:, k, m * 128:(m + 1) * 128], hT[:, k, cs]) for k in range(KC)], [blkB] + HTB, [pB])
            return p, pB

        def tm_block(blk, blkB, c):
            p, pB = next_pm()
            ts = slice(H + c * 128, H + (c + 1) * 128)
            mmg(p[:, :], [(hT[:, k, ts], blk[:, k, :]) for k in range(KC)], [blkB] + HTB, [pB])
            return p, pB

        def hgrn_prep_head(h, first_tile):
            A("dve", lambda e: e.tensor_scalar(out=flog[:], in0=sig[:], scalar1=lbv[:, 1, h:h + 1], scalar2=lbv[:, 0, h:h + 1],
                                               op0=ALU.mult, op1=ALU.add), r=["sig", "lbv"], w=["flog"])
            A("dve", lambda e: e.tensor_scalar(out=flog[:], in0=flog[:], scalar1=1e-20, scalar2=None, op0=ALU.max), r=["flog"], w=["flog"])
            A("act", lambda e: e.activation(out=flog[:], in_=flog[:], func=AF.Ln), r=["flog"], w=["flog"])
            A("pool", lambda e: e.tensor_scalar(out=kf[:], in0=sig[:], scalar1=lbv[:, 2, h:h + 1], scalar2=lbv[:, 1, h:h + 1],
                                                op0=ALU.mult, op1=ALU.add), r=["sig", "lbv"], w=["kf"])
            A("dve", lambda e: e.tensor_tensor_scan(out=b128[:], data0=rst128[:], data1=flog[:], initial=0.0, op0=ALU.mult, op1=ALU.add),
              r=["rst128", "flog"], w=["b128"])
            b3 = b128[:].rearrange("p (c t) -> p c t", t=128)
            A("dve", lambda e: e.scalar_tensor_tensor(out=d1[:].rearrange("p (c t) -> p c t", t=128), in0=b3, scalar=-1.0,
                                                      in1=b3[:, :, 127:128].to_broadcast([128, NCH, 128]), op0=ALU.mult, op1=ALU.add),
              r=["b128"], w=["d1"])
            A("act", lambda e: e.activation(out=d1[:], in_=d1[:], func=AF.Exp), r=["d1"], w=["d1"])
            A("pool", lambda e: e.tensor_tensor(out=kdT[:], in0=kf[:], in1=d1[:], op=ALU.mult), r=["kf", "d1"], w=["kdT"])
            A("act", lambda e: e.activation(out=decs[:], in_=b3[:, :, 127], func=AF.Exp), r=["b128"], w=["decs"])
            A("dve", lambda e: e.reduce_sum(out=blsum[:], in_=b3[:, :, 127], axis=AX.X), r=["b128"], w=["blsum"])
            A("dve", lambda e: e.tensor_tensor(out=Lacc[:, h:h + 1], in0=Lacc[:, h:h + 1], in1=blsum[:], op=ALU.add),
              r=["blsum", "Lacc"], w=["Lacc"])
            for c in range(NCH):
                A("pe", lambda e, c=c: e.transpose(out=ptr[:, c * 128:(c + 1) * 128], in_=kdT[:, c * 128:(c + 1) * 128], identity=ident[:]),
                  r=["kdT", "ident"], w=["ptr"])
            A("act", lambda e: e.copy(out=kd_sb[:], in_=ptr[:, 0:NCH * 128].rearrange("p (c k) -> p c k", k=128)), r=["ptr"], w=["kd_sb"])

        def state_update(h, c):
            mmg(pS[:, h * 128:(h + 1) * 128], [(kd_sb[:, c, :], ivtok[:, c, h * 128:(h + 1) * 128])], ["kd_sb", "ivtok"], ["pS"])
            Sh = S[:, h * 128:(h + 1) * 128]
            A("dve", lambda e: e.scalar_tensor_tensor(out=Sh, in0=Sh, scalar=decs[:, c:c + 1], in1=pS[:, h * 128:(h + 1) * 128],
                                                      op0=ALU.mult, op1=ALU.add), r=["S", "decs", "pS"], w=["S"])
            A("act", lambda e: e.copy(out=Sbf[:, h * 128:(h + 1) * 128], in_=Sh), r=["S"], w=["Sbf"])

        out_ops = []

        def stage(n):
            if STOP <= n:
                raise StopBuild()

        try:
          for i in range(NT):
            c0 = i * T
            first = i == 0
            A("sp", lambda e, c0=c0: e.dma_start(out=xt[:], in_=xT[:, c0:c0 + TW].rearrange("(k p) t -> p k t", p=128)),
              w=["xt"], key="x")
            if full:
                norm_front(0, 1, TW, 0)
            else:
                norm_front(0, 1, T, H)
            cur = slice(H, TW)
            allc = slice(0, TW)

            if not full:
                blk, blkB = wnext()
                sigs = []
                fzp = []
                for m in range(4):
                    p, pB = fm_block(blk, blkB, m, cur, T)
                    A("act", lambda e, p=p, m=m: e.activation(out=sigA[:, m, :], in_=p[:, 0:T], func=AF.Sigmoid), r=[pB], w=[f"sigA{m}"])
                blk, blkB = wnext()
                for c in range(NCH):
                    p, pB = tm_block(blk, blkB, c)
                    A("act", lambda e, p=p, c=c: e.copy(out=ivtok[:, c, :], in_=p[:, :]), r=[pB], w=["ivtok"])
                for h in range(4):
                    A("dve", lambda e, h=h: e.tensor_copy(out=sig[:], in_=sigA[:, h, :]), r=[f"sigA{h}"], w=["sig"])
                    hgrn_prep_head(h, first)
                    for c in range(NCH):
                        state_update(h, c)
                continue

            stage(1)
            zaf, sgl, hpad, uf, cacc = F4[0], F4[1], F4[2], F4[3], F4[4]
            blk, blkB = wnext()
            for m in range(4):
                p, pB = fm_block(blk, blkB, m, allc, TW)
                A("act", lambda e, p=p, m=m: e.copy(out=zaf[:, m, :], in_=p[:, 0:TW]), r=[pB], w=[f"zaf{m}"])
                if first:
                    A("pool", lambda e, m=m: e.tensor_scalar(out=zaf[:, m, 0:H], in0=zaf[:, m, 0:H], scalar1=nfT[:, 0:1], scalar2=None,
                                                             op0=ALU.mult), r=[f"zaf{m}", "nfT"], w=[f"zaf{m}"])
            stage(2)
            for g in range(4):
                win = 2 << g
                src = zaf[:, g, :]
                srcB = f"zaf{g}"
                dlist = [1 << j for j in range(g + 1)]
                lo = 0
                cur_ap, curB = src, srcB
                for di, dd in enumerate(dlist):
                    lo += dd
                    dst = F1[di % 2]
                    dstB = f"F1_{di % 2}"
                    A("dve", lambda e, dst=dst, ca=cur_ap, lo=lo, dd=dd: e.tensor_tensor(out=dst[:, lo:TW], in0=ca[:, lo:TW],
                                                                                         in1=ca[:, lo - dd:TW - dd], op=ALU.add),
                      r=[curB], w=[dstB])
                    cur_ap, curB = dst[:, :], dstB
                if first:
                    A("dve", lambda e, ca=cur_ap, g=g: e.tensor_tensor(out=ca[:, H:H + 16], in0=ca[:, H:H + 16], in1=corr[:, g, :], op=ALU.mult),
                      r=[curB, "corr"], w=[curB])
                pooled = B4[0]
                A("dve", lambda e, ca=cur_ap, g=g, win=win: e.scalar_tensor_tensor(out=pooled[:, g, 0:T], in0=ca[:, H:TW], scalar=1.0 / win,
                                                                                   in1=zaf[:, g, H:TW], op0=ALU.mult, op1=ALU.subtract),
                  r=[curB, f"zaf{g}"], w=[f"pooled{g}"])
                p, pB = next_pm()
                mmg(p[:, 0:T], [(poolw[:, g, :], pooled[:, g, 0:T])], ["poolw", f"pooled{g}"], [pB])
                A("act", lambda e, p=p, g=g: e.activation(out=ymix[:, g, :], in_=p[:, 0:T], func=AF.Copy, scale=vecs[:, 64 + g:65 + g]),
                  r=[pB, "vecs"], w=[f"ymix{g}"])
            stage(3)
            blk, blkB = wnext()
            for m in range(4):
                p, pB = fm_block(blk, blkB, m, cur, T)
                A("act", lambda e, p=p, m=m: e.copy(out=uf[:, m, 0:T], in_=p[:, 0:T]), r=[pB], w=[f"uf{m}"])
            stage(4)
            blk, blkB = wnext()
            for c in range(NCH):
                p, pB = tm_block(blk, blkB, c)
                A("act", lambda e, p=p: e.copy(out=vraw[:], in_=p[:, :]), r=[pB], w=["vraw"])
                stage(4.05)
                A("dve", lambda e: e.reduce_sum(out=st4[:, 0:1], in_=vraw[:], axis=AX.X), r=["vraw"], w=["st4"])
                stage(4.07)
                A("act", lambda e: e.activation(out=vsq[:], in_=vraw[:], func=AF.Square), r=["vraw"], w=["vsq"])
                stage(4.1)
                A("dve", lambda e: e.reduce_sum(out=st4[:, 1:2], in_=vsq[:], axis=AX.X), r=["vsq"], w=["st4"])
                A("dve", lambda e: e.tensor_scalar(out=st4[:, 2:4], in0=st4[:, 0:2], scalar1=1.0 / 512, scalar2=None, op0=ALU.mult),
                  r=["st4"], w=["st4"])
                A("dve", lambda e: e.tensor_tensor(out=st4[:, 4:5], in0=st4[:, 2:3], in1=st4[:, 2:3], op=ALU.mult), r=["st4"], w=["st4"])
                A("dve", lambda e: e.tensor_tensor(out=st4[:, 5:6], in0=st4[:, 3:4], in1=st4[:, 4:5], op=ALU.subtract), r=["st4"], w=["st4"])
                A("act", lambda e: e.activation(out=st4[:, 6:7], in_=st4[:, 5:6], func=AF.Sqrt, bias=epsT[:, 0:1], scale=1.0),
                  r=["st4", "epsT"], w=["st4"])
                A("dve", lambda e: e.reciprocal(out=st4[:, 7:8], in_=st4[:, 6:7]), r=["st4"], w=["st4"])
                stage(4.2)
                A("dve", lambda e, c=c: e.tensor_scalar(out=vn[:, c, :], in0=vraw[:], scalar1=st4[:, 2:3], scalar2=st4[:, 7:8],
                                                        op0=ALU.subtract, op1=ALU.mult), r=["vraw", "st4"], w=["vn"])
            stage(4.3)
            for h in range(4):
                p, pB = next_pm()
                for c in range(NCH):
                    mmg(p[:, c * 128:(c + 1) * 128], [(vn[:, c, h * 128:(h + 1) * 128], wmT[:, h, :])], ["vn", "wmT"], [pB])
                t1 = F1[2]
                stage(4.4)
                A("dve", lambda e, p=p, h=h: e.scalar_tensor_tensor(out=t1[:, 0:T].rearrange("p (c t) -> p c t", t=128),
                                                                    in0=p[:, 0:T].rearrange("p (c t) -> p c t", t=128),
                                                                    scalar=vecs[:, 68 + h:69 + h],
                                                                    in1=Csg[:, h:h + 1, :].to_broadcast([128, NCH, 128]),
                                                                    op0=ALU.mult, op1=ALU.add),
                  r=[pB, "vecs", "Csg"], w=["F1_2"])
                stage(4.5)
                A("pool", lambda e, h=h: e.tensor_tensor(out=ymix[:, 4 + h, :], in0=t1[:, 0:T], in1=uf[:, h, 0:T], op=ALU.mult),
                  r=["F1_2", f"uf{h}"], w=[f"ymix{4 + h}"])
            stage(5)
            blk, blkB = wnext()
            for m in range(4):
                p, pB = fm_block(blk, blkB, m, allc, TW)
                A("act", lambda e, p=p, m=m: e.activation(out=sgl[:, m, :], in_=p[:, 0:TW], func=AF.Sigmoid), r=[pB], w=[f"sgl{m}"])
            blk, blkB = wnext()
            for m in range(4):
                p, pB = fm_block(blk, blkB, m, allc, TW)
                A("dve", lambda e, p=p, m=m: e.tensor_tensor(out=hpad[:, m, :], in0=p[:, 0:TW], in1=sgl[:, m, :], op=ALU.mult),
                  r=[pB, f"sgl{m}"], w=[f"hpad{m}"])
                if first:
                    A("pool", lambda e, m=m: e.tensor_scalar(out=hpad[:, m, 0:H], in0=hpad[:, m, 0:H], scalar1=nfT[:, 0:1], scalar2=None,
                                                             op0=ALU.mult), r=[f"hpad{m}", "nfT"], w=[f"hpad{m}"])
            stage(6)
            for m in range(4):
                eng = "dve"
                A(eng, lambda e, m=m: e.tensor_scalar(out=cacc[:, m, 0:T], in0=hpad[:, m, 2:2 + T], scalar1=dwT[:, m, 0:1],
                                                      scalar2=vecs[:, 76 + m:77 + m], op0=ALU.mult, op1=ALU.add),
                  r=[f"hpad{m}", "dwT", "vecs"], w=[f"cacc{m}"])
                for k in range(1, 31):
                    A(eng, lambda e, m=m, k=k: e.scalar_tensor_tensor(out=cacc[:, m, 0:T], in0=hpad[:, m, 2 + k:2 + k + T],
                                                                      scalar=dwT[:, m, k:k + 1], in1=cacc[:, m, 0:T],
                                                                      op0=ALU.mult, op1=ALU.add),
                      r=[f"hpad{m}", "dwT", f"cacc{m}"], w=[f"cacc{m}"])
            cbf, csq = B4[1], B4[2]
            for m in range(4):
                A("act", lambda e, m=m: e.copy(out=cbf[:, m, 0:T], in_=cacc[:, m, 0:T]), r=[f"cacc{m}"], w=[f"cbf{m}"])
                A("act", lambda e, m=m: e.activation(out=csq[:, m, 0:T], in_=cacc[:, m, 0:T], func=AF.Square), r=[f"cacc{m}"], w=[f"csq{m}"])
            mmg(pst[:, 0:T], [(ones[:], cbf[:, m, 0:T]) for m in range(4)], ["ones"] + [f"cbf{m}" for m in range(4)], ["pst"])
            mmg(pst[:, T:2 * T], [(ones[:], csq[:, m, 0:T]) for m in range(4)], ["ones"] + [f"csq{m}" for m in range(4)], ["pst"])
            mean, msq, crs = F1[0], F1[1], F1[3]
            A("act", lambda e: e.activation(out=mean[:, 0:T], in_=pst[:, 0:T], func=AF.Copy, scale=1.0 / 512), r=["pst"], w=["F1_0"])
            A("pool", lambda e: e.tensor_tensor(out=msq[:, 0:T], in0=mean[:, 0:T], in1=mean[:, 0:T], op=ALU.mult), r=["F1_0"], w=["F1_1"])
            A("dve", lambda e: e.scalar_tensor_tensor(out=crs[:, 0:T], in0=pst[:, T:2 * T], scalar=1.0 / 512, in1=msq[:, 0:T],
                                                      op0=ALU.mult, op1=ALU.subtract), r=["pst", "F1_1"], w=["F1_3"])
            A("act", lambda e: e.activation(out=crs[:, 0:T], in_=crs[:, 0:T], func=AF.Sqrt, bias=epsT[:, 0:1], scale=1.0),
              r=["F1_3", "epsT"], w=["F1_3"])
            A("dve", lambda e: e.reciprocal(out=crs[:, 0:T], in_=crs[:, 0:T]), r=["F1_3"], w=["F1_3"])
            hs = B4[0]
            for m in range(4):
                A("dve", lambda e, m=m: e.tensor_tensor(out=cacc[:, m, 0:T], in0=cacc[:, m, 0:T], in1=mean[:, 0:T], op=ALU.subtract),
                  r=[f"cacc{m}", "F1_0"], w=[f"cacc{m}"])
                A("pool", lambda e, m=m: e.tensor_tensor(out=cacc[:, m, 0:T], in0=cacc[:, m, 0:T], in1=crs[:, 0:T], op=ALU.mult),
                  r=[f"cacc{m}", "F1_3"], w=[f"cacc{m}"])
                A("act", lambda e, m=m: e.activation(out=hs[:, m, 0:T], in_=cacc[:, m, 0:T], func=AF.Silu, bias=vecs[:, 84 + m:85 + m],
                                                     scale=vecs[:, 80 + m:81 + m]), r=[f"cacc{m}", "vecs"], w=[f"pooled{m}"])
            for m in range(4):
                p, pB = next_pm()
                mmg(p[:, 0:T], [(pwb[:, k, m * 128:(m + 1) * 128], hs[:, k, 0:T]) for k in range(4)],
                    ["pwb"] + [f"pooled{k}" for k in range(4)], [pB])
                A("act", lambda e, p=p, m=m: e.activation(out=ymix[:, 8 + m, :], in_=p[:, 0:T], func=AF.Identity,
                                                          bias=vecs[:, 88 + m:89 + m], scale=1.0), r=[pB, "vecs"], w=[f"ymix{8 + m}"])
            stage(7)
            qA, sigA_, sogA = F4[0], F4[1], F4[2]
            blk, blkB = wnext()
            for m in range(4):
                p, pB = fm_block(blk, blkB, m, cur, T)
                A("act", lambda e, p=p, m=m: e.activation(out=qA[:, m, 0:T], in_=p[:, 0:T], func=AF.Copy, scale=128.0 ** -0.5),
                  r=[pB], w=[f"zaf{m}"])
            blk, blkB = wnext()
            for m in range(4):
                p, pB = fm_block(blk, blkB, m, cur, T)
                A("act", lambda e, p=p, m=m: e.activation(out=sigA_[:, m, 0:T], in_=p[:, 0:T], func=AF.Sigmoid), r=[pB], w=[f"sgl{m}"])
            blk, blkB = wnext()
            for c in range(NCH):
                p, pB = tm_block(blk, blkB, c)
                A("act", lambda e, p=p, c=c: e.copy(out=ivtok[:, c, :], in_=p[:, :]), r=[pB], w=["ivtok"])
            blk, blkB = wnext()
            for m in range(4):
                p, pB = fm_block(blk, blkB, m, cur, T)
                A("act", lambda e, p=p, m=m: e.activation(out=sogA[:, m, 0:T], in_=p[:, 0:T], func=AF.Silu), r=[pB], w=[f"hpad{m}"])
            for h in range(4):
                A("dve", lambda e, h=h: e.tensor_copy(out=sig[:], in_=sigA_[:, h, 0:T]), r=[f"sgl{h}"], w=["sig"])
                hgrn_prep_head(h, first)
                b3 = b128[:].rearrange("p (c t) -> p c t", t=128)
                kf3 = kf[:].rearrange("p (c t) -> p c t", t=128)
                A("dve", lambda e: e.tensor_tensor_scan(out=b16[:], data0=rst16[:], data1=flog[:], initial=0.0, op0=ALU.mult, op1=ALU.add),
                  r=["rst16", "flog"], w=["b16"])
                A("act", lambda e: e.activation(out=b16[:], in_=b16[:], func=AF.Exp), r=["b16"], w=["b16"])
                A("pool", lambda e, h=h: e.tensor_tensor(out=Qrel[:], in0=qA[:, h, 0:T], in1=b16[:], op=ALU.mult), r=["b16", f"zaf{h}"], w=["Qrel"])
                A("act", lambda e: e.activation(out=qs[:], in_=b128[:], func=AF.Exp), r=["b128"], w=["qs"])
                A("pool", lambda e, h=h: e.tensor_tensor(out=Qc[:], in0=qA[:, h, 0:T], in1=qs[:], op=ALU.mult), r=["qs", f"zaf{h}"], w=["Qc"])
                for I in range(8):
                    W_ = 16 * (I + 1)
                    if I == 0:
                        A("dve", lambda e: e.tensor_scalar(out=kt1[:, :, 0:16], in0=b3[:, :, 0:16], scalar1=-1.0, scalar2=60.0,
                                                           op0=ALU.mult, op1=ALU.min), r=["b128"], w=["kt1"])
                    else:
                        A("dve", lambda e, I=I, W_=W_: e.scalar_tensor_tensor(out=kt1[:, :, 0:W_], in0=b3[:, :, 0:W_], scalar=-1.0,
                                                                             in1=b3[:, :, 16 * I - 1:16 * I].to_broadcast([128, NCH, W_]),
                                                                             op0=ALU.mult, op1=ALU.add), r=["b128"], w=["kt1"])
                        A("dve", lambda e, W_=W_: e.tensor_scalar(out=kt1[:, :, 0:W_], in0=kt1[:, :, 0:W_], scalar1=60.0, scalar2=None,
                                                                  op0=ALU.min), r=["kt1"], w=["kt1"])
                    A("act", lambda e, W_=W_: e.activation(out=kt2[:, :, 0:W_], in_=kt1[:, :, 0:W_], func=AF.Exp), r=["kt1"], w=["kt2"])
                    A("pool", lambda e, I=I, W_=W_: e.tensor_tensor(out=Kref[:, I, :, 0:W_], in0=kt2[:, :, 0:W_], in1=kf3[:, :, 0:W_], op=ALU.mult),
                      r=["kt2", "kf"], w=["Kref"])
                stage(8)
                for c in range(NCH):
                    def fn(e, c=c):
                        ins = None
                        for I in range(8):
                            W_ = 16 * (I + 1)
                            ins = e.matmul(patt[0:W_, c * 128 + 16 * I:c * 128 + 16 * I + 16], lhsT=Kref[:, I, c, 0:W_],
                                           rhs=Qrel[:, c * 128 + 16 * I:c * 128 + 16 * I + 16], start=True, stop=True)
                        return ins
                    A("pe", fn, r=["Kref", "Qrel"], w=["patt"])
                A("dve", lambda e: e.tensor_tensor(out=attT[:], in0=patt[:, 0:T].rearrange("p (c t) -> p c t", t=128),
                                                   in1=triM[:, None, :].to_broadcast([128, NCH, 128]), op=ALU.mult),
                  r=["patt", "triM"], w=["attT"])
                pob = po[h // 2]
                poB = f"po{h // 2}"
                for c in range(NCH):
                    oc = (h % 2) * T + c * 128
                    mmg(pob[:, oc:oc + 128], [(Sbf[:, h * 128:(h + 1) * 128], Qc[:, c * 128:(c + 1) * 128]),
                                               (ivtok[:, c, h * 128:(h + 1) * 128], attT[:, c, :])],
                        ["Sbf", "Qc", "ivtok", "attT"], [poB])
                    state_update(h, c)
                oh = pob[:, (h % 2) * T:(h % 2) * T + T]
                A("act", lambda e, oh=oh: e.activation(out=osq[:], in_=oh, func=AF.Square), r=[poB], w=["osq"])
                mmg(pst[:, 0:T], [(ones[:], osq[:])], ["ones", "osq"], ["pst"])
                rh = F1[3]
                A("act", lambda e: e.activation(out=rh[:, 0:T], in_=pst[:, 0:T], func=AF.Sqrt, bias=epsT[:, 0:1], scale=1.0 / 128),
                  r=["pst", "epsT"], w=["F1_3"])
                A("dve", lambda e: e.reciprocal(out=rh[:, 0:T], in_=rh[:, 0:T]), r=["F1_3"], w=["F1_3"])
                t1 = F1[2]
                A("dve", lambda e, oh=oh, h=h: e.scalar_tensor_tensor(out=t1[:, 0:T], in0=oh, scalar=vecs[:, 100 + h:101 + h], in1=rh[:, 0:T],
                                                                      op0=ALU.mult, op1=ALU.mult), r=[poB, "vecs", "F1_3"], w=["F1_2"])
                A("pool", lambda e, h=h: e.tensor_tensor(out=ymix[:, 12 + h, :], in0=t1[:, 0:T], in1=sogA[:, h, 0:T], op=ALU.mult),
                  r=["F1_2", f"hpad{h}"], w=[f"ymix{12 + h}"])
            if first:
                dbg("ymix", ymix[:], [128, KC, T], [f"ymix{k}" for k in range(KC)], BF16)
            YMB = [f"ymix{k}" for k in range(KC)]
            stage(9)
            for cb in range(4):
                blk, blkB = wnext()
                for m in range(4):
                    p, pB = next_pm()
                    kk = 4 * cb + m
                    mmg(p[:, 0:T], [(blk[:, k, m * 128:(m + 1) * 128], ymix[:, k, :]) for k in range(KC)], [blkB] + YMB, [pB])
                    A("dve", lambda e, p=p, kk=kk: e.tensor_copy(out=yacc[:, kk, :], in_=p[:, 0:T]), r=[pB], w=[f"yacc{kk}"])
                    A("act", lambda e, kk=kk: e.activation(out=sq[:, kk, 0:T], in_=yacc[:, kk, :], func=AF.Square), r=[f"yacc{kk}"], w=[f"sq{kk // 4}"])

            def post_residual(gti, dst_is_out):
                mmg(pst[:, 0:T], [(ones[:], sq[:, k, 0:T]) for k in range(KC)], ["ones"] + [f"sq{g}" for g in range(4)], ["pst"])
                A("act", lambda e: e.activation(out=rstd[:, 0:T], in_=pst[:, 0:T], func=AF.Sqrt, bias=epsT[:, 0:1], scale=1.0 / D),
                  r=["pst", "epsT"], w=["rstd"])
                A("dve", lambda e: e.reciprocal(out=rstd[:, 0:T], in_=rstd[:, 0:T]), r=["rstd"], w=["rstd"])
                for k in range(KC):
                    A("dve", lambda e, k=k: e.scalar_tensor_tensor(out=yacc[:, k, :], in0=yacc[:, k, :], scalar=mods[:, gti, k:k + 1],
                                                                   in1=rstd[:, 0:T], op0=ALU.mult, op1=ALU.mult),
                      r=[f"yacc{k}", "mods", "rstd"], w=[f"yacc{k}"])
                    A("pool", lambda e, k=k: e.tensor_tensor(out=xt[:, k, H:TW], in0=xt[:, k, H:TW], in1=yacc[:, k, :], op=ALU.add),
                      r=[f"yacc{k}", "xt"], w=["xt"])

            post_residual(2, False)
            if first:
                dbg("xmid", xt[:, :, H:TW], [128, KC, T], ["xt"])
            stage(10)
            norm_front(3, 4, T, H)
            for q in range(4):
                hq = hidq[q % 2]
                for cbw in range(4):
                    blk, blkB = wnext()
                    for m in range(4):
                        p, pB = fm_block(blk, blkB, m, cur, T)
                        hc = 4 * cbw + m
                        r_ = rl[hc % 4]
                        if hc % 2 == 0:
                            A("act", lambda e, p=p, r_=r_: e.activation(out=r_[:], in_=p[:, 0:T], func=AF.Relu), r=[pB], w=[f"rl{hc % 4}"])
                        else:
                            A("dve", lambda e, p=p, r_=r_: e.tensor_scalar(out=r_[:], in0=p[:, 0:T], scalar1=0.0, scalar2=None, op0=ALU.max),
                              r=[pB], w=[f"rl{hc % 4}"])
                        A("pool", lambda e, r_=r_, hq=hq, hc=hc: e.tensor_tensor(out=hq[:, hc, :], in0=r_[:], in1=r_[:], op=ALU.mult),
                          r=[f"rl{hc % 4}"], w=[f"hid{q % 2}_{hc}"])
                HQB = [f"hid{q % 2}_{k}" for k in range(KC)]
                stage(11 + q * 2)
                for cb in range(4):
                    blk, blkB = wnext()
                    for m in range(4):
                        p, pB = next_pm()
                        kk = 4 * cb + m
                        mmg(p[:, 0:T], [(blk[:, k, m * 128:(m + 1) * 128], hq[:, k, :]) for k in range(KC)], [blkB] + HQB, [pB])
                        if q == 0:
                            A("act", lambda e, p=p, kk=kk: e.copy(out=yacc[:, kk, :], in_=p[:, 0:T]), r=[pB], w=[f"yacc{kk}"])
                        else:
                            A("dve", lambda e, p=p, kk=kk: e.tensor_tensor(out=yacc[:, kk, :], in0=yacc[:, kk, :], in1=p[:, 0:T], op=ALU.add),
                              r=[pB, f"yacc{kk}"], w=[f"yacc{kk}"])
                        if q == 3:
                            A("act", lambda e, kk=kk: e.activation(out=sq[:, kk, 0:T], in_=yacc[:, kk, :], func=AF.Square),
                              r=[f"yacc{kk}"], w=[f"sq{kk // 4}"])
            stage(20)
            post_residual(5, True)
            for k in range(KC):
                out_ops.append(A("sp", lambda e, i=i, k=k: e.dma_start(out=xo_d[k * 128:(k + 1) * 128, i * T:(i + 1) * T],
                                                                       in_=xt[:, k, H:TW]), r=["xt"], key=f"xo{k % 4}"))

        except StopBuild:
            pass
        if not full:
            out_ops.append(A("sp", lambda e: e.dma_start(out=So_d, in_=S[:]), r=["S"], key="so"))
            out_ops.append(A("sp", lambda e: e.dma_start(out=Lo_d, in_=Lacc[:]), r=["Lacc"], key="lo"))
        out_ops.extend(dbg_outs.values())
        s.emit(final_wait_ops=out_ops)
    return nc


def _cols(v):
    v = np.asarray(v, np.float32)
    return np.ascontiguousarray(v.reshape(-1, 128).T)


_CACHE = {}


def _get(mode, NT=TOK // T, debug=False):
    key = (mode, NT, debug)
    if key not in _CACHE:
        _CACHE[key] = build(mode, NT, debug)
    return _CACHE[key]


def layer_inputs(l, inp):
    vec = np.concatenate([
        _cols(inp["norm_mix_pre"][l]), _cols(inp["norm_mix_post"][l]), _cols(inp["norm_mlp_pre"][l]), _cols(inp["norm_mlp_post"][l]),
        _cols(inp["pool_scale"][l]), _cols(inp["sgu_norm_g"][l]), _cols(inp["sgu_norm_b"][l]), _cols(inp["conv_dw_b"][l]),
        _cols(inp["conv_norm_g"][l]), _cols(inp["conv_norm_b"][l]), _cols(inp["conv_pw_b"][l]),
        _cols(inp["hgrn_lb_logits"][0]), _cols(inp["hgrn_lb_logits"][1]), _cols(inp["hgrn_norm_g"][l])], axis=1)
    assert vec.shape == (128, NV)
    com = {
        "ada_w": np.ascontiguousarray(inp["ada_w"][l]),
        "ada_bT": _cols(inp["ada_b"][l]),
        "vecs": np.ascontiguousarray(vec),
        "lsel": np.full((128, 1), float(l), np.float32),
        "w_in": np.ascontiguousarray(inp["w_in"][l]),
    }
    fullw = {
        "w_out": np.ascontiguousarray(inp["w_out"][l]),
        "w1": np.ascontiguousarray(inp["mlp_w1"][l]),
        "w2": np.ascontiguousarray(inp["mlp_w2"][l]),
        "pool_w": np.ascontiguousarray(np.transpose(inp["pool_w"][l], (1, 0, 2))),
        "sgu_wT": np.ascontiguousarray(np.transpose(inp["sgu_w"][l], (2, 0, 1))),
        "sgu_bb": np.ascontiguousarray(np.broadcast_to(inp["sgu_b"][l][None], (128, 4, 128))),
        "dwT": np.ascontiguousarray(np.transpose(inp["conv_dw"][l].T.reshape(4, 128, 31), (1, 0, 2))),
        "conv_pw": np.ascontiguousarray(inp["conv_pw"][l]),
    }
    return com, fullw


def kernel(**inp):
    inp = {k: np.asarray(v, np.float32) for k, v in inp.items()}
    x = inp["x"]
    nseg = SEQ // TOK
    xTs = []
    for core in range(NCORE):
        b, j = divmod(core, nseg)
        xTs.append(np.ascontiguousarray(x[b, j * TOK:(j + 1) * TOK, :].T))
    cTs = [_cols(inp["c"][core // nseg]) for core in range(NCORE)]
    corr_first = np.ones((4, 16), np.float32)
    for g in range(4):
        win = 2 << g
        for t in range(16):
            corr_first[g, t] = win / min(t + 1, win)
    for l in range(2):
        com, fullw = layer_inputs(l, inp)
        xh = []
        for core in range(NCORE):
            b, j = divmod(core, nseg)
            halo = np.zeros((D, H), np.float32) if j == 0 else xTs[core - 1][:, TOK - H:]
            xh.append(np.ascontiguousarray(np.concatenate([halo, xTs[core]], axis=1)))
        ncA = _get("A")
        resA = run_bass_kernel_spmd(ncA, [dict(com, xT=xh[c], cT=cTs[c]) for c in range(NCORE)], core_ids=list(range(NCORE)))
        So = [np.asarray(r["So"], np.float32) for r in resA.results]
        Lo = [np.asarray(r["Lo"], np.float32) for r in resA.results]
        maps = []
        for core in range(NCORE):
            b, j = divmod(core, nseg)
            Sp = np.zeros((3, 128, 512), np.float32)
            Lp = np.zeros((3, 128, 4), np.float32)
            for m in range(j):
                Sp[m] = So[b * nseg + m]
                Lp[m] = Lo[b * nseg + m]
            first = j == 0
            cr = corr_first if first else np.ones((4, 16), np.float32)
            maps.append(dict(com, **fullw, xT=xh[core], cT=cTs[core], Sp=Sp, Lp=Lp,
                             nf=np.full((128, 1), 0.0 if first else 1.0, np.float32),
                             corr=np.ascontiguousarray(np.broadcast_to(cr[None], (128, 4, 16)))))
        ncB = _get("B")
        resB = run_bass_kernel_spmd(ncB, maps, core_ids=list(range(NCORE)))
        xTs = [np.asarray(r["xo"], np.float32) for r in resB.results]
    out = np.empty((BATCH, SEQ, D), np.float32)
    for core in range(NCORE):
        b, j = divmod(core, nseg)
        out[b, j * TOK:(j + 1) * TOK, :] = xTs[core].T
    return out
```
